# Optimizing a Trainium2 kernel written in Bass

```python
import math
import jax, jax.numpy as jnp
from jax import lax
import numpy as np

D_MODEL = 1024
BATCH = 16
SEQ = 2048
DEPTH = 4

CHUNK = 64
LEFT_CHUNKS = 8
BAND = (LEFT_CHUNKS + 1) * CHUNK
ATT_HEADS = 8
ATT_HEAD_DIM = 64
ATT_WIDTH = ATT_HEADS * ATT_HEAD_DIM
REL_CLIP = 256
REL_TABLE = REL_CLIP + CHUNK
CONV_WIDTH = D_MODEL // 2
CONV_K = 31
FFN_DIM = 2816
FFN_CONV_K = 3
N_BRANCH = 2
IN_COLS = 3 * ATT_WIDTH + 2 * CONV_WIDTH + N_BRANCH * D_MODEL
SPLITS = [ATT_WIDTH, 2 * ATT_WIDTH, 3 * ATT_WIDTH, 3 * ATT_WIDTH + 2 * CONV_WIDTH]
ALPHA = (2 * DEPTH) ** 0.25
BETA = (8 * DEPTH) ** -0.25
LN_EPS = 1e-5
NEG_INF = -1e30

kernel_name = "hybrid_chunk_attn_conformer_conv_deepnorm"


def layer_norm(x, g, b):
    xf = x.astype(jnp.float32)
    mu = jnp.mean(xf, axis=-1, keepdims=True)
    xc = xf - mu
    var = jnp.mean(xc * xc, axis=-1, keepdims=True)
    y = xc * lax.rsqrt(var + LN_EPS) * g.astype(jnp.float32) + b.astype(jnp.float32)
    return y.astype(x.dtype)


def causal_dwconv(x, w, b):
    k, c = w.shape
    y = lax.conv_general_dilated(
        x, w[:, None, :].astype(x.dtype), window_strides=(1,),
        padding=((k - 1, 0),), dimension_numbers=("NWC", "WIO", "NWC"),
        feature_group_count=c)
    return y + b.astype(x.dtype)


def chunk_attention(q, k, v, rel_bias):
    bsz, seq, nh, dh = q.shape
    nc = seq // CHUNK
    pad = LEFT_CHUNKS * CHUNK
    kp = jnp.pad(k, ((0, 0), (pad, 0), (0, 0), (0, 0)))
    vp = jnp.pad(v, ((0, 0), (pad, 0), (0, 0), (0, 0)))
    q_ch = q.reshape(bsz, nc, CHUNK, nh, dh).transpose(1, 0, 2, 3, 4)
    qi = jnp.arange(CHUNK)[:, None]
    kj = jnp.arange(BAND)[None, :]
    rel = qi + pad - kj
    idx = jnp.clip(rel, -(CHUNK - 1), REL_CLIP) + (CHUNK - 1)
    bias = rel_bias.astype(jnp.float32)[:, idx]
    scale = 1.0 / math.sqrt(dh)

    def one_chunk(args):
        n, q_n = args
        start = n * CHUNK
        k_n = lax.dynamic_slice_in_dim(kp, start, BAND, axis=1)
        v_n = lax.dynamic_slice_in_dim(vp, start, BAND, axis=1)
        s = jnp.einsum("bqhd,bkhd->bhqk", q_n, k_n).astype(jnp.float32) * scale + bias
        valid = (kj + start) >= pad
        s = jnp.where(valid[None, None], s, NEG_INF)
        p = jax.nn.softmax(s, axis=-1).astype(v.dtype)
        return jnp.einsum("bhqk,bkhd->bqhd", p, v_n)

    out = lax.map(one_chunk, (jnp.arange(nc), q_ch))
    return out.transpose(1, 0, 2, 3, 4).reshape(bsz, seq, nh * dh)


def conformer_conv(u, w_dw, b_dw, g_ln, b_ln):
    a, g = jnp.split(u, 2, axis=-1)
    h = a * jax.nn.sigmoid(g)
    h = causal_dwconv(h, w_dw, b_dw)
    h = layer_norm(h, g_ln, b_ln)
    return jax.nn.silu(h)


def conv_ffn(x, w_up, w_dw, b_dw, w_down):
    u = x @ w_up
    a, b = jnp.split(u, 2, axis=-1)
    a = causal_dwconv(a, w_dw, b_dw)
    return (jax.nn.gelu(a) * b) @ w_down


def setup_inputs(seed: int = 0) -> dict:
    key = jax.random.key(seed)
    ks = jax.random.split(key, 20)
    L, D = DEPTH, D_MODEL

    def nrm(k, shape, fan_in, scale=1.0):
        return jax.random.normal(k, shape, jnp.float32) * (scale * fan_in ** -0.5)

    def small(k, shape, s=0.02):
        return s * jax.random.normal(k, shape, jnp.float32)

    col_scale = jnp.ones((IN_COLS,), jnp.float32).at[2 * ATT_WIDTH:3 * ATT_WIDTH].set(BETA)
    return {
        "x": jax.random.normal(ks[0], (BATCH, SEQ, D), jnp.float32),
        "w_in": nrm(ks[1], (L, D, IN_COLS), D) * col_scale,
        "b_in": small(ks[2], (L, IN_COLS)),
        "rel_bias": small(ks[3], (L, ATT_HEADS, REL_TABLE), 0.1),
        "w_att_out": nrm(ks[4], (L, ATT_WIDTH, D), ATT_WIDTH, BETA),
        "conv_w": nrm(ks[5], (L, CONV_K, CONV_WIDTH), CONV_K),
        "conv_b": small(ks[6], (L, CONV_WIDTH)),
        "conv_ln_g": 1.0 + small(ks[7], (L, CONV_WIDTH)),
        "conv_ln_b": small(ks[8], (L, CONV_WIDTH)),
        "w_conv_out": nrm(ks[9], (L, CONV_WIDTH, D), CONV_WIDTH, BETA),
        "w_o": nrm(ks[10], (L, D, D), D, BETA),
        "ln1_g": 1.0 + small(ks[11], (L, D)),
        "ln1_b": small(ks[12], (L, D)),
        "w_up": nrm(ks[13], (L, D, 2 * FFN_DIM), D, BETA),
        "ffn_conv_w": nrm(ks[14], (L, FFN_CONV_K, FFN_DIM), FFN_CONV_K),
        "ffn_conv_b": small(ks[15], (L, FFN_DIM)),
        "w_down": nrm(ks[16], (L, FFN_DIM, D), FFN_DIM, BETA),
        "ln2_g": 1.0 + small(ks[17], (L, D)),
        "ln2_b": small(ks[18], (L, D)),
    }


def reference(x, w_in, b_in, rel_bias, w_att_out, conv_w, conv_b, conv_ln_g, conv_ln_b,
              w_conv_out, w_o, ln1_g, ln1_b, w_up, ffn_conv_w, ffn_conv_b, w_down,
              ln2_g, ln2_b):
    bsz, seq, _ = x.shape
    for l in range(DEPTH):
        h = x @ w_in[l] + b_in[l]
        q, k, v, conv_in, gate_logits = jnp.split(h, SPLITS, axis=-1)
        q = q.reshape(bsz, seq, ATT_HEADS, ATT_HEAD_DIM)
        k = k.reshape(bsz, seq, ATT_HEADS, ATT_HEAD_DIM)
        v = v.reshape(bsz, seq, ATT_HEADS, ATT_HEAD_DIM)
        y_att = chunk_attention(q, k, v, rel_bias[l]) @ w_att_out[l]
        y_conv = conformer_conv(conv_in, conv_w[l], conv_b[l], conv_ln_g[l],
                                conv_ln_b[l]) @ w_conv_out[l]
        g_att, g_conv = jnp.split(jax.nn.sigmoid(gate_logits), N_BRANCH, axis=-1)
        mix = (g_att * y_att + g_conv * y_conv) @ w_o[l]
        x = layer_norm(ALPHA * x + mix, ln1_g[l], ln1_b[l])
        ffn = conv_ffn(x, w_up[l], ffn_conv_w[l], ffn_conv_b[l], w_down[l])
        x = layer_norm(ALPHA * x + ffn, ln2_g[l], ln2_b[l])
    return x
```

```python
import contextlib
import numpy as np
import concourse.bass as bass
import concourse.mybir as mybir
from concourse.bass_utils import run_bass_kernel_spmd

F32 = mybir.dt.float32
BF16 = mybir.dt.bfloat16
AF = mybir.ActivationFunctionType
ALU = mybir.AluOpType

ENGS = ("pe", "act", "dve", "pool", "sp")


class Prog:
    N_DMA_SEMS = 24

    def __init__(self, nc, es):
        self.nc = nc
        self.ops = {e: [] for e in ENGS}
        self.cnt = {e: 0 for e in ENGS}
        self.sem = {}
        for e in ("pe", "act", "dve", "pool"):
            self.sem[e] = es.enter_context(nc.semaphore("s_" + e))
        self.dma_cnt = []
        for i in range(self.N_DMA_SEMS):
            self.sem[("dma", i)] = es.enter_context(nc.semaphore("s_dma%d" % i))
            self.dma_cnt.append(0)
        self.next_dma = [0, 0]
        self.waited = {e: {} for e in ENGS}
        self.res = {}
        self.final_waits = {}
        self.tag = ""
        self.tags = {e: [] for e in ENGS}

    def _deps(self, reads, writes, eng=None):
        deps = {}

        def add(k, v):
            if deps.get(k, 0) < v:
                deps[k] = v

        for r in reads:
            st = self.res.get(r)
            if st is not None and st["w"] is not None:
                add(*st["w"])
            if st is not None and r.startswith("ps"):
                for k, v in st["r"].items():
                    if k != eng:
                        add(k, v)
        for w in writes:
            st = self.res.get(w)
            if st is not None:
                if st["w"] is not None:
                    add(*st["w"])
                for k, v in st["r"].items():
                    add(k, v)
        return deps

    def _waits(self, eng, deps):
        waits = []
        for k, v in deps.items():
            if k == eng and eng == "pe":
                continue
            if self.waited[eng].get(k, 0) >= v:
                continue
            self.waited[eng][k] = v
            waits.append((k, v))
        return waits

    def _record(self, key, val, reads, writes):
        for r in reads:
            st = self.res.setdefault(r, {"w": None, "r": {}})
            if st["r"].get(key, 0) < val:
                st["r"][key] = val
        for w in writes:
            self.res[w] = {"w": (key, val), "r": {}}

    def op(self, eng, fn, reads=(), writes=()):
        waits = self._waits(eng, self._deps(reads, writes, eng))
        self.cnt[eng] += 1
        val = self.cnt[eng]
        self.ops[eng].append((waits, fn, (eng, 1)))
        self.tags[eng].append(self.tag)
        self._record(eng, val, reads, writes)

    def dma(self, queue, out_ap, in_ap, reads=(), writes=(), final=False):
        half = self.N_DMA_SEMS // 2
        qi = 0 if queue == "pool" else 1
        i = qi * half + self.next_dma[qi]
        self.next_dma[qi] = (self.next_dma[qi] + 1) % half
        key = ("dma", i)
        deps = self._deps(reads, writes, queue)
        if self.dma_cnt[i] > 0 and deps.get(key, 0) < self.dma_cnt[i]:
            deps[key] = self.dma_cnt[i]
        waits = self._waits(queue, deps)
        self.dma_cnt[i] += 16
        val = self.dma_cnt[i]

        def fn(e, out_ap=out_ap, in_ap=in_ap, queue=queue):
            if queue == "pool":
                return e.dma_start(out=out_ap, in_=in_ap, max_dma_last_dim=4096)
            return e.dma_start(out=out_ap, in_=in_ap)

        self.ops[queue].append((waits, fn, (key, 16)))
        self._record(key, val, reads, writes)
        if final:
            self.final_waits[key] = val

    def emit(self):
        nc = self.nc
        fw = [(k, v) for k, v in self.final_waits.items()]
        sem = self.sem
        ops = self.ops

        def run(eng_name, e):
            for waits, fn, (k, inc) in ops[eng_name]:
                for wk, wv in waits:
                    e.wait_ge(sem[wk], wv)
                fn(e).then_inc(sem[k], inc)

        with nc.Block() as block:

            @block.tensor
            def _(e):
                run("pe", e)

            @block.scalar
            def _(e):
                run("act", e)

            @block.vector
            def _(e):
                run("dve", e)

            @block.gpsimd
            def _(e):
                run("pool", e)

            @block.sync
            def _(e):
                run("sp", e)
                for k, v in fw:
                    e.wait_ge(sem[k], v)


D = 1024
NL = 4
SEQ = 2048
TT = 512
NT = TT // 128
ALPHA = 8.0 ** 0.25
LN_EPS = 1e-5
NFF = 22
NV = 292
NEG = -30000.0
GELU_C = 0.7978845608028654

V_BIN = 0
V_CONVB = 36
V_CLNG = 40
V_CLNB = 44
V_LN1G = 48
V_LN1B = 56
V_LN2G = 64
V_LN2B = 72
V_FCB = 80
V_FCW = 102
V_CW = 168


class _Stop(Exception):
    pass


NTPL = 34


def build_program(L=NL, NST=8, st_per_seq=4, stop=None, plan=None):
    nc = bass.Bass("TRN2", target_bir_lowering=False)
    ntok = NST * TT
    x_d = nc.dram_tensor("x", [ntok, D], F32, kind="ExternalInput").ap()
    y_d = nc.dram_tensor("y", [ntok, D], F32, kind="ExternalOutput").ap()
    w_in_d = nc.dram_tensor("w_in", [NL, D, 4608], F32, kind="ExternalInput").ap()
    w_ao_d = nc.dram_tensor("w_att_out", [NL, 512, D], F32, kind="ExternalInput").ap()
    w_co_d = nc.dram_tensor("w_conv_out", [NL, 512, D], F32, kind="ExternalInput").ap()
    w_o_d = nc.dram_tensor("w_o", [NL, D, D], F32, kind="ExternalInput").ap()
    w_up_d = nc.dram_tensor("w_up", [NL, D, 5632], F32, kind="ExternalInput").ap()
    w_dn_d = nc.dram_tensor("w_down", [NL, 2816, D], F32, kind="ExternalInput").ap()
    vecs_d = nc.dram_tensor("vecs", [128, NL * NV], F32, kind="ExternalInput").ap()
    bias_d = nc.dram_tensor("biasT", [NL, 128, 8 * 640], F32, kind="ExternalInput").ap()
    lnf_d = nc.dram_tensor("lnf", [2, 128, D], F32, kind="ExternalInput").ap()
    id_d = nc.dram_tensor("ident", [128, 128], F32, kind="ExternalInput").ap()
    scr_d = nc.dram_tensor("wscr", [NL * NTPL, 128, 4096], BF16).ap()
    dram = {"w_in": w_in_d, "w_att_out": w_ao_d, "w_conv_out": w_co_d, "w_o": w_o_d, "w_up": w_up_d,
            "w_down": w_dn_d}
    plan_out = []

    with contextlib.ExitStack() as es:
        p = Prog(nc, es)

        def sb(name, shape, dt):
            return es.enter_context(nc.sbuf_tensor("sb_" + name, shape, dt))

        xT32 = sb("xT32", [128, 8, TT], F32)
        xTb = sb("xTb", [128, 8, TT], BF16)
        kTh = sb("kTh", [128, NL, 4, 512], BF16)
        vh = sb("vh", [128, NL * 32, 65], BF16)
        hct = sb("hct", [128, NL, 4, 30], BF16)
        fft = sb("fft", [128, NL, NFF, 2], F32)
        vecs = sb("vecs", [128, NL * NV], F32)
        biasT = sb("biasT", [128, 8, 640], BF16)
        ident = sb("ident", [128, 128], F32)
        lnf = sb("lnf", [128, 2, D], F32)
        epsT = sb("epsT", [128, 1], F32)
        A_Q, A_K, A_C, A_M, A_AT = 0, 4096, 6144, 8192, 12288
        A_HC = 14336
        A_V = A_HC + 4 * 542 + 8
        A_D = A_V + 4 * 520
        A_END = A_D + 31 * 128
        arena = sb("arena", [128, A_END], BF16)

        def units(lo, hi):
            return ["u%d" % i for i in range(lo // 512, (hi - 1) // 512 + 1)]

        def qT(j, hh):
            o = A_Q + hh * 2048 + j * 512
            return arena[:, o:o + 512]

        def kT(j):
            return arena[:, A_K + j * 512:A_K + (j + 1) * 512]

        def cT(j):
            return arena[:, A_C + j * 512:A_C + (j + 1) * 512]

        def mT(j):
            return arena[:, A_M + j * 512:A_M + (j + 1) * 512]

        def aT(j):
            return arena[:, A_AT + j * 512:A_AT + (j + 1) * 512]

        def hc(j):
            return arena[:, A_HC + j * 542:A_HC + (j + 1) * 542]

        def vcur(t):
            return arena[:, A_V + t * 520:A_V + (t + 1) * 520].rearrange("p (h d) -> p h d", d=65)

        def gT(f):
            return arena[:, f * 512:(f + 1) * 512]

        Dg = arena[:, A_D:A_END].rearrange("p (k m) -> p k m", m=128)
        R_Q = lambda j: units(A_Q + j * 512, A_Q + (j + 1) * 512) + units(A_Q + 2048 + j * 512, A_Q + 2048 + (j + 1) * 512)
        R_K = lambda j: units(A_K + j * 512, A_K + (j + 1) * 512)
        R_C = lambda j: units(A_C + j * 512, A_C + (j + 1) * 512)
        R_M = lambda j: units(A_M + j * 512, A_M + (j + 1) * 512)
        R_AT = lambda j: units(A_AT + j * 512, A_AT + (j + 1) * 512)
        R_HC = lambda j: units(A_HC + j * 542, A_HC + (j + 1) * 542)
        R_V = lambda t: units(A_V + t * 520, A_V + (t + 1) * 520)
        R_D = units(A_D, A_END)
        R_G = lambda f: units(f * 512, (f + 1) * 512)
        R_VALL = units(A_V, A_V + 4 * 520)

        NTMP = 4
        tmp = [sb("tmp%d" % i, [128, 512], F32) for i in range(NTMP)]
        id8 = sb("id8", [128, 128], BF16)
        lgt = [sb("lg%d" % i, [128, 640], F32) for i in range(3)]
        PT = [sb("PT%d" % i, [128, 640], BF16) for i in range(4)]
        h2 = sb("h2", [128, 4, TT], F32)
        xn = [sb("xn%d" % i, [128, D], F32) for i in range(2)]
        atok = [sb("atok%d" % i, [128, 512], F32) for i in range(2)]
        st6 = sb("st6", [128, 4, 2, 6], F32)
        mv = sb("mv", [128, 4, 2], F32)
        rs = sb("rs", [128, 4, 3], F32)
        rc = sb("rc", [128, 8], F32)
        NSLOT = 4
        wsl = [sb("wsl%d" % i, [128, 4096], BF16) for i in range(NSLOT)]
        ps = es.enter_context(nc.psum_tensor("ps", [128, 8, 512], F32))

        cnt = {"tmp": 0, "bank": 0, "pair": 0, "lg": 0, "pt": 0, "xn": 0, "atok": 0, "slot": 0}

        def rot(key, n):
            i = cnt[key] % n
            cnt[key] = (i + 1) % n
            return i

        ring = {"banks": 6}

        def bank():
            b = rot("bank", ring["banks"])
            return ps[:, b, :], ["ps%d" % b]

        def pair():
            b = 2 * rot("pair", ring["banks"] // 2)
            return ps[:, b:b + 2, :], ["ps%d" % b, "ps%d" % (b + 1)]

        def tmpb():
            i = rot("tmp", NTMP)
            return tmp[i], ["tmp%d" % i]

        def MM(out, lhsT, rhs, start, stop, reads, writes):
            p.op("pe", lambda e: e.matmul(out, lhsT, rhs, start=start, stop=stop), reads, writes)

        def TR(out, in_, reads, writes):
            p.op("pe", lambda e: e.transpose(out, in_, ident[:]), list(reads) + ["ident"], writes)

        def ACT(out, in_, func, reads, writes, bias=None, scale=None):
            kw = {}
            if bias is not None:
                kw["bias"] = bias
            if scale is not None:
                kw["scale"] = scale
            p.op("act", lambda e: e.activation(out=out, in_=in_, func=func, **kw), reads, writes)

        def TS(out, in0, s1, s2, op0, op1, reads, writes, eng="dve"):
            if s2 is None:
                p.op(eng, lambda e: e.tensor_scalar(out=out, in0=in0, scalar1=s1, scalar2=None, op0=op0),
                     reads, writes)
            else:
                p.op(eng, lambda e: e.tensor_scalar(out=out, in0=in0, scalar1=s1, scalar2=s2, op0=op0, op1=op1),
                     reads, writes)

        def STT(out, in0, scalar, in1, op0, op1, reads, writes):
            p.op("dve", lambda e: e.scalar_tensor_tensor(out=out, in0=in0, scalar=scalar, in1=in1, op0=op0, op1=op1),
                 reads, writes)

        def TTo(out, in0, in1, op, reads, writes, eng="dve"):
            p.op(eng, lambda e: e.tensor_tensor(out=out, in0=in0, in1=in1, op=op), reads, writes)

        def CP(out, in_, reads, writes, eng="dve"):
            p.op(eng, lambda e: e.tensor_copy(out=out, in_=in_), reads, writes)

        def MS(ap, val, writes, eng="dve"):
            p.op(eng, lambda e: e.memset(ap, val), (), writes)

        def vcol(l, c):
            return vecs[:, l * NV + c:l * NV + c + 1]

        pending = []

        class Task:
            def __init__(self, loads, compute):
                self.loads = loads
                self.compute = compute

        tasks = []
        ntask = [0]

        gstate = {"issued": 0, "done": 0, "st": 0, "l": 0, "ti": 0}

        def src_of(spec):
            off, shape, name, l_, c0, c1 = spec
            return dram[name][l_][:, c0:c1].rearrange("(k p) n -> p k n", p=128)

        def issue_entry(ent, s):
            used = 0
            for (off, shape, name, l_, c0, c1) in ent["specs"]:
                n = shape[1] * shape[2]
                used = max(used, off + n)
            idx = ent["l"] * NTPL + ent["ti"]
            if ent["st"] == 0:
                for spec in ent["specs"]:
                    off, shape = spec[0], spec[1]
                    n = shape[1] * shape[2]
                    dst = wsl[s][:, off:off + n].rearrange("p (a b) -> p a b", b=shape[2])
                    p.dma("pool", dst, src_of(spec), reads=(), writes=["w%d" % s])
                p.dma("sp", scr_d[idx][:, 0:used], wsl[s][:, 0:used], reads=["w%d" % s], writes=["scr%d" % idx])
            else:
                p.dma("sp", wsl[s][:, 0:used], scr_d[idx][:, 0:used], reads=["scr%d" % idx], writes=["w%d" % s])

        def run_tasks():
            for tk in tasks:
                gi = gstate["done"]
                if plan is None:
                    ent = {"specs": tk.loads, "st": gstate["st"], "l": gstate["l"], "ti": gstate["ti"]}
                    plan_out.append(ent)
                    issue_entry(ent, gi % NSLOT)
                else:
                    while gstate["issued"] < min(len(plan), gi + NSLOT):
                        k = gstate["issued"]
                        issue_entry(plan[k], k % NSLOT)
                        gstate["issued"] += 1
                s_ = gi % NSLOT
                tk.compute(wsl[s_], ["w%d" % s_])
                gstate["done"] += 1
                gstate["ti"] += 1
                ntask[0] += 1
                if stop == "T%d" % ntask[0]:
                    raise _Stop()
            del tasks[:]


        p.dma("sp", vecs[:], vecs_d, writes=["vecs"])
        p.dma("sp", ident[:], id_d, writes=["ident"])
        p.dma("sp", lnf[:], lnf_d.rearrange("a p n -> p a n"), writes=["lnf"])
        MS(epsT[:], LN_EPS, ["eps"])
        TS(id8[:], ident[:], 8.0, None, ALU.mult, None, ["ident"], ["id8"])
        MS(vh[:, :, 64:65], 1.0, ["vh%d" % l for l in range(NL)])

        def layer_norm_phase(l, gcol, bcol, final_tok0=None):
            st_ = {}

            def I(t):
                cols = slice(t * 128, (t + 1) * 128)
                pa, pan = pair()
                pav = pa.rearrange("p a n -> p (a n)")
                for j in range(8):
                    TR(pav[:, j * 128:(j + 1) * 128], xT32[:, j, cols], ["x32_%d_%d" % (j, t)], pan)
                st_[t] = (pav, pan)

            def S(t):
                pav, pan = st_[t]
                p.op("dve", lambda e: e.bn_stats(out=st6[:, t, 0, :], in_=pav[:, 0:512]), pan, ["st6a%d" % t])
                p.op("dve", lambda e: e.bn_stats(out=st6[:, t, 1, :], in_=pav[:, 512:1024]), pan, ["st6b%d" % t])
                p.op("dve", lambda e: e.bn_aggr(out=mv[:, t, :], in_=st6[:, t, :, :].rearrange("p a s -> p (a s)")),
                     ["st6a%d" % t, "st6b%d" % t], ["mv%d" % t])
                ACT(rs[:, t, 0:1], mv[:, t, 1:2], AF.Sqrt, ["mv%d" % t, "eps"], ["rsa%d" % t],
                    bias=epsT[:, 0:1], scale=1.0)
                p.op("dve", lambda e: e.reciprocal(out=rs[:, t, 1:2], in_=rs[:, t, 0:1]), ["rsa%d" % t], ["rsb%d" % t])
                xi = rot("xn", 2)
                TS(rs[:, t, 2:3], mv[:, t, 0:1], rs[:, t, 1:2], -1.0, ALU.mult, ALU.mult,
                   ["mv%d" % t, "rsb%d" % t], ["rsc%d" % t])
                ACT(xn[xi][:], pav, AF.Identity, pan + ["rsb%d" % t, "rsc%d" % t], ["xn%d" % xi],
                    bias=rs[:, t, 2:3], scale=rs[:, t, 1:2])
                st_[t] = xi

            def B(t):
                xi = st_[t]
                pb, pbn = pair()
                pbv = pb.rearrange("p a n -> p (a n)")
                for j in range(8):
                    TR(pbv[:, j * 128:(j + 1) * 128], xn[xi][:, j * 128:(j + 1) * 128], ["xn%d" % xi],
                       pbn[j // 4:j // 4 + 1])
                st_[t] = (pbv, pbn)

            def E(t):
                cols = slice(t * 128, (t + 1) * 128)
                pbv, pbn = st_[t]
                for j in range(4):
                    TS(xT32[:, j, cols], pbv[:, j * 128:(j + 1) * 128], vcol(l, gcol + j), vcol(l, bcol + j),
                       ALU.mult, ALU.add, pbn[0:1] + ["vecs"], ["x32_%d_%d" % (j, t)])
                for j in range(4, 8):
                    ACT(xT32[:, j, cols], pbv[:, j * 128:(j + 1) * 128], AF.Identity, pbn[1:2] + ["vecs"],
                        ["x32_%d_%d" % (j, t)], bias=vcol(l, bcol + j), scale=vcol(l, gcol + j))
                CP(xTb[:, 0:4, cols], xT32[:, 0:4, cols], ["x32_%d_%d" % (j, t) for j in range(4)], R_XB[0:4])
                ACT(xTb[:, 4:8, cols], xT32[:, 4:8, cols], AF.Copy, ["x32_%d_%d" % (j, t) for j in range(4, 8)],
                    R_XB[4:8])

            def F(t):
                xi = st_[t]
                r0 = final_tok0 + t * 128
                TTo(xn[xi][:], xn[xi][:], lnf[:, 0, :], ALU.mult, ["xn%d" % xi, "lnf"], ["xn%d" % xi])
                TTo(xn[xi][:], xn[xi][:], lnf[:, 1, :], ALU.add, ["xn%d" % xi, "lnf"], ["xn%d" % xi])
                p.dma("sp", y_d[r0:r0 + 128, :], xn[xi][:], reads=["xn%d" % xi], final=True)

            if final_tok0 is None:
                for f, t in [(I, 0), (I, 1), (I, 2), (S, 0), (B, 0), (S, 1), (I, 3), (S, 2), (B, 1), (E, 0),
                             (B, 2), (S, 3), (E, 1), (B, 3), (E, 2), (E, 3)]:
                    f(t)
            else:
                for f, t in [(I, 0), (I, 1), (I, 2), (S, 0), (F, 0), (S, 1), (F, 1), (I, 3), (S, 2), (F, 2),
                             (S, 3), (F, 3)]:
                    f(t)

        R_XB = ["xb_%d" % j for j in range(8)]

        def check(name):
            if stop == name:
                raise _Stop()

        try:
          for st in range(NST):
              pos = st % st_per_seq
              B0 = pos * NT
              tok0 = st * TT
              p.tag = 'X'
              for t in range(NT):
                  xi = rot("xn", 2)
                  p.dma("sp", xn[xi][:], x_d[tok0 + t * 128:tok0 + (t + 1) * 128, :], writes=["xn%d" % xi])
                  pb, pbn = pair()
                  pbv = pb.rearrange("p a n -> p (a n)")
                  for j in range(8):
                      TR(pbv[:, j * 128:(j + 1) * 128], xn[xi][:, j * 128:(j + 1) * 128], ["xn%d" % xi], pbn)
                  cols = slice(t * 128, (t + 1) * 128)
                  CP(xT32[:, :, cols], pb.rearrange("p a (b n) -> p (a b) n", n=128), pbn,
                     ["x32_%d_%d" % (j, t) for j in range(8)])
                  ACT(xTb[:, :, cols], xT32[:, :, cols], AF.Copy, ["x32_%d_%d" % (j, t) for j in range(8)], R_XB)

              check('X')
              for l in range(L):
                  last = (l == L - 1)
                  gstate["st"], gstate["l"], gstate["ti"] = st, l, 0
                  p.dma("pool", biasT[:].rearrange("p h n -> p (h n)"), bias_d[l], writes=["biasT"])
                  if pos == 0:
                      MS(hc(0)[:, 0:30], 0.0, R_HC(0))
                      for j in range(1, 4):
                          MS(hc(j)[:, 0:30], 0.0, R_HC(j))
                      MS(fft[:, l, :, :], 0.0, ["fft%d" % l])
                  else:
                      for j in range(4):
                          CP(hc(j)[:, 0:30], hct[:, l, j, :], ["hct%d" % l], R_HC(j))
                  for t in range(NT):
                      MS(vcur(t)[:, :, 64:65], 1.0, R_V(t))
                  check('S')

                  p.tag = 'A'
                  def proj_fm(dstf, rdst, bcol0):
                      def compute(W, wr):
                          Wv = W[:, 0:4096].rearrange("p (k n) -> p k n", n=512)
                          for j in range(4):
                              b_, bn_ = bank()
                              for k in range(8):
                                  MM(b_, Wv[:, k, j * 128:(j + 1) * 128], xTb[:, k, :], k == 0, k == 7,
                                     wr + ["xb_%d" % k], bn_)
                              ACT(dstf(j), b_, AF.Identity, bn_ + ["vecs"], rdst(j), bias=vcol(l, bcol0 + j))
                      return compute

                  def q_compute(W, wr):
                      Wv = W[:, 0:4096].rearrange("p (k n) -> p k n", n=512)
                      for j in range(4):
                          b_, bn_ = bank()
                          for k in range(8):
                              MM(b_, Wv[:, k, j * 128:(j + 1) * 128], xTb[:, k, :], k == 0, k == 7,
                                 wr + ["xb_%d" % k], bn_)
                          for hh in range(2):
                              r = slice(hh * 64, hh * 64 + 64)
                              ACT(qT(j, hh)[r, :], b_[r, :], AF.Identity, bn_ + ["vecs"], R_Q(j),
                                  bias=vecs[r, l * NV + V_BIN + j:l * NV + V_BIN + j + 1])

                  MS(arena[64:128, A_Q:A_Q + 2048], 0.0, [u for j in range(4) for u in R_Q(j)])
                  MS(arena[0:64, A_Q + 2048:A_Q + 4096], 0.0, [u for j in range(4) for u in R_Q(j)])
                  tasks.append(Task([(0, [128, 8, 512], "w_in", l, 0, 512)], q_compute))
                  tasks.append(Task([(0, [128, 8, 512], "w_in", l, 512, 1024)], proj_fm(kT, R_K, V_BIN + 4)))

                  def v_compute(W, wr):
                      Wv = W[:, 0:4096].rearrange("p (k n) -> p k n", n=512)
                      for t in range(NT):
                          b_, bn_ = bank()
                          for k in range(8):
                              MM(b_, xTb[:, k, t * 128:(t + 1) * 128], Wv[:, k, :], k == 0, k == 7,
                                 wr + ["xb_%d" % k], bn_)
                          CP(vcur(t)[:, :, 0:64], b_.rearrange("p (h d) -> p h d", d=64), bn_, R_V(t))

                  tasks.append(Task([(0, [128, 8, 512], "w_in", l, 1024, 1536)], v_compute))

                  def glu_compute(W, wr):
                      Wa = W[:, 0:2048].rearrange("p (k n) -> p k n", n=256)
                      Wg = W[:, 2048:4096].rearrange("p (k n) -> p k n", n=256)
                      return Wa, Wg

                  for half in range(2):
                      def c_compute(W, wr, half=half):
                          Wa = W[:, 0:2048].rearrange("p (k n) -> p k n", n=256)
                          Wg = W[:, 2048:4096].rearrange("p (k n) -> p k n", n=256)
                          for jj in range(2):
                              j = half * 2 + jj
                              ba, ban = bank()
                              bg, bgn = bank()
                              for k in range(8):
                                  MM(ba, Wa[:, k, jj * 128:(jj + 1) * 128], xTb[:, k, :], k == 0, k == 7,
                                     wr + ["xb_%d" % k], ban)
                              for k in range(8):
                                  MM(bg, Wg[:, k, jj * 128:(jj + 1) * 128], xTb[:, k, :], k == 0, k == 7,
                                     wr + ["xb_%d" % k], bgn)
                              tg, tgn = tmpb()
                              ACT(tg[:], bg, AF.Sigmoid, bgn + ["vecs"], tgn, bias=vcol(l, V_BIN + 16 + j))
                              STT(hc(j)[:, 30:542], ba, vcol(l, V_BIN + 12 + j), tg[:], ALU.add, ALU.mult,
                                  ban + tgn + ["vecs"], R_HC(j))

                      c0 = 1536 + half * 256
                      tasks.append(Task([(0, [128, 8, 256], "w_in", l, c0, c0 + 256),
                                         (2048, [128, 8, 256], "w_in", l, c0 + 512, c0 + 768)], c_compute))
                  run_tasks()

                  check('A')
                  if pos < st_per_seq - 1:
                      for j in range(4):
                          CP(hct[:, l, j, :], hc(j)[:, 512:542], R_HC(j), ["hct%d" % l])

                  def conv_steps():
                      NDR = 7
                      for half in range(2):
                          accs = [(ps[:, 4 + jj, :], ["ps%d" % (4 + jj)]) for jj in range(2)]
                          for k in range(31):
                              sl = k % NDR
                              for jj in range(2):
                                  j = half * 2 + jj
                                  dt_ = Dg[:, sl * 4 + jj, :]
                                  TS(dt_, ident[:], vcol(l, V_CW + j * 31 + k), None, ALU.mult, None,
                                     ["ident", "vecs"], ["Dg%d_%d" % (sl, jj)])
                                  MM(accs[jj][0], dt_, hc(j)[:, k:k + 512], k == 0, k == 30,
                                     ["Dg%d_%d" % (sl, jj)] + R_HC(j), accs[jj][1])
                              yield
                          for jj in range(2):
                              j = half * 2 + jj
                              ACT(h2[:, j, :], accs[jj][0], AF.Identity, accs[jj][1] + ["vecs"], ["h2_%d" % j],
                                  bias=vcol(l, V_CONVB + j))
                          yield

                  conv_it = conv_steps()

                  def conv_step(n=1):
                      for _ in range(n):
                          try:
                              next(conv_it)
                          except StopIteration:
                              return

                  p.tag = 'B2'
                  ring["banks"] = 4
                  for g in range(NT):
                      gb = B0 + g
                      p_lo = max(0, 4 - gb)
                      c_lo = p_lo * 128
                      qcols = slice(g * 128, (g + 1) * 128)
                      O, On = ps[:, 6:8, :], ["ps6", "ps7"]
                      Ov = [O[:, hh, 0:260].rearrange("p (h d) -> p h d", d=65) for hh in range(2)]

                      def pv(h, pti):
                          for pp in range(p_lo, 5):
                              rel = gb - 4 + pp - B0
                              if rel < 0:
                                  vsrc = vh[:, (l * 4 + rel + 4) * 8 + h, :]
                                  vr = ["vh%d" % l]
                              else:
                                  vsrc = vcur(rel)[:, h, :]
                                  vr = R_V(rel)
                              MM(Ov[h // 4][:, h % 4, :], PT[pti][:, pp * 128:(pp + 1) * 128], vsrc,
                                 pp == p_lo, pp == 4, ["PT%d" % pti] + vr, On)

                      def qk(h):
                          hp, hh = h // 2, h % 2
                          S, Sn = pair()
                          Sv = S.rearrange("p a n -> p (a n)")
                          for pp in range(p_lo, 5):
                              rel = gb - 4 + pp - B0
                              if rel < 0:
                                  ksrc = kTh[:, l, hp, (rel + 4) * 128:(rel + 5) * 128]
                                  kr = ["kTh%d" % l]
                              else:
                                  ksrc = kT(hp)[:, rel * 128:(rel + 1) * 128]
                                  kr = R_K(hp)
                              blk = Sv[:, pp * 128:(pp + 1) * 128]
                              MM(blk, ksrc, qT(hp, hh)[:, qcols], True, True, kr + R_Q(hp), Sn)
                          li = rot("lg", 3)
                          STT(lgt[li][:, c_lo:640], Sv[:, c_lo:640], 0.125, biasT[:, h, c_lo:640], ALU.mult, ALU.add,
                              Sn + ["biasT"], ["lg%d" % li])
                          pti = rot("pt", 4)
                          ACT(PT[pti][:, c_lo:640], lgt[li][:, c_lo:640], AF.Exp, ["lg%d" % li], ["PT%d" % pti])
                          return (h, pti)

                      SKEW = 3
                      inflight = [qk(h) for h in range(SKEW)]
                      for h in range(8):
                          if h + SKEW < 8:
                              inflight.append(qk(h + SKEW))
                          conv_step(2)
                          pv(*inflight.pop(0))
                      ai = rot("atok", 2)
                      for hh in range(2):
                          p.op("dve", lambda e, hh=hh: e.reciprocal(out=rc[:, hh * 4:(hh + 1) * 4], in_=Ov[hh][:, :, 64]),
                               On, ["rc%d" % hh])
                          TTo(atok[ai][:, hh * 256:(hh + 1) * 256].rearrange("p (h d) -> p h d", d=64),
                              Ov[hh][:, :, 0:64],
                              rc[:, hh * 4:(hh + 1) * 4].unsqueeze(2).broadcast_to([128, 4, 64]),
                              ALU.mult, On + ["rc%d" % hh], ["atok%d_%d" % (ai, hh)])
                      b_, bn_ = bank()
                      for c in range(4):
                          TR(b_[:, c * 128:(c + 1) * 128], atok[ai][:, c * 128:(c + 1) * 128],
                             ["atok%d_0" % ai, "atok%d_1" % ai], bn_)
                      for c in range(4):
                          ACT(aT(c)[:, qcols], b_[:, c * 128:(c + 1) * 128], AF.Identity, bn_ + ["vecs"], R_AT(c),
                              bias=vcol(l, V_BIN + 8 + c))
                  conv_step(100)
                  ring["banks"] = 6
                  check('B2')
                  p.tag = 'B1'
                  cst = {}

                  def cI(t):
                      cols = slice(t * 128, (t + 1) * 128)
                      b_, bn_ = bank()
                      for j in range(4):
                          TR(b_[:, j * 128:(j + 1) * 128], h2[:, j, cols], ["h2_%d" % j], bn_)
                      cst[t] = (b_, bn_)

                  def cS(t):
                      b_, bn_ = cst[t]
                      p.op("dve", lambda e: e.bn_stats(out=st6[:, t, 0, :], in_=b_), bn_, ["st6a%d" % t])
                      p.op("dve", lambda e: e.bn_aggr(out=mv[:, t, :], in_=st6[:, t, 0, :]), ["st6a%d" % t], ["mv%d" % t])
                      ACT(rs[:, t, 0:1], mv[:, t, 1:2], AF.Sqrt, ["mv%d" % t, "eps"], ["rsa%d" % t],
                          bias=epsT[:, 0:1], scale=1.0)
                      p.op("dve", lambda e: e.reciprocal(out=rs[:, t, 1:2], in_=rs[:, t, 0:1]), ["rsa%d" % t], ["rsb%d" % t])
                      tn, tnn = tmpb()
                      TS(tn[:], b_, mv[:, t, 0:1], rs[:, t, 1:2], ALU.subtract, ALU.mult,
                         bn_ + ["mv%d" % t, "rsb%d" % t], tnn)
                      cst[t] = (tn, tnn)

                  def cB(t):
                      tn, tnn = cst[t]
                      b2, b2n = bank()
                      for j in range(4):
                          TR(b2[:, j * 128:(j + 1) * 128], tn[:, j * 128:(j + 1) * 128], tnn, b2n)
                      cst[t] = (b2, b2n)

                  def cE(t):
                      cols = slice(t * 128, (t + 1) * 128)
                      b2, b2n = cst[t]
                      for j in range(4):
                          ACT(cT(j)[:, cols], b2[:, j * 128:(j + 1) * 128], AF.Silu, b2n + ["vecs"], R_C(j),
                              bias=vcol(l, V_CLNB + j), scale=vcol(l, V_CLNG + j))

                  for f, t in [(cI, 0), (cI, 1), (cI, 2), (cI, 3), (cS, 0), (cB, 0), (cS, 1), (cB, 1), (cE, 0),
                               (cS, 2), (cB, 2), (cE, 1), (cS, 3), (cB, 3), (cE, 2), (cE, 3)]:
                      f(t)

                  if pos < st_per_seq - 1:
                      for j in range(4):
                          CP(kTh[:, l, j, :], kT(j), R_K(j), ["kTh%d" % l])
                      for t in range(NT):
                          CP(vh[:, (l * 4 + t) * 8:(l * 4 + t + 1) * 8, 0:64], vcur(t)[:, :, 0:64], R_V(t), ["vh%d" % l])

                  p.tag = 'CD'
                  for j in range(8):
                      def m_compute(W, wr, j=j):
                          Wao = W[:, 0:512].rearrange("p (k n) -> p k n", n=128)
                          Wco = W[:, 512:1024].rearrange("p (k n) -> p k n", n=128)
                          Wga = W[:, 1024:2048].rearrange("p (k n) -> p k n", n=128)
                          Wgc = W[:, 2048:3072].rearrange("p (k n) -> p k n", n=128)
                          ya, yan = bank()
                          yc, ycn = bank()
                          za, zan = bank()
                          zc, zcn = bank()
                          for k in range(4):
                              MM(ya, Wao[:, k, :], aT(k), k == 0, k == 3, wr + R_AT(k), yan)
                          for k in range(4):
                              MM(yc, Wco[:, k, :], cT(k), k == 0, k == 3, wr + R_C(k), ycn)
                          for k in range(8):
                              MM(za, Wga[:, k, :], xTb[:, k, :], k == 0, k == 7, wr + ["xb_%d" % k], zan)
                          for k in range(8):
                              MM(zc, Wgc[:, k, :], xTb[:, k, :], k == 0, k == 7, wr + ["xb_%d" % k], zcn)
                          s1, s1n = tmpb()
                          s2, s2n = tmpb()
                          ACT(s1[:], za, AF.Sigmoid, zan + ["vecs"], s1n, bias=vcol(l, V_BIN + 20 + j))
                          ACT(s2[:], zc, AF.Sigmoid, zcn + ["vecs"], s2n, bias=vcol(l, V_BIN + 28 + j))
                          TTo(s1[:], ya, s1[:], ALU.mult, yan + s1n, s1n)
                          TTo(s2[:], yc, s2[:], ALU.mult, ycn + s2n, s2n)
                          TTo(mT(j), s1[:], s2[:], ALU.add, s1n + s2n, R_M(j))

                      tasks.append(Task([
                          (0, [128, 4, 128], "w_att_out", l, j * 128, (j + 1) * 128),
                          (512, [128, 4, 128], "w_conv_out", l, j * 128, (j + 1) * 128),
                          (1024, [128, 8, 128], "w_in", l, 2560 + j * 128, 2560 + (j + 1) * 128),
                          (2048, [128, 8, 128], "w_in", l, 3584 + j * 128, 3584 + (j + 1) * 128),
                      ], m_compute))

                  def resid_proj(nk, srcf, rsrc):
                      def mk(jq):
                          def compute(W, wr):
                              Wv = W[:, 0:nk * 512].rearrange("p (k n) -> p k n", n=512) if nk == 8 else None
                              for jj in range(4):
                                  j = jq * 4 + jj
                                  b_, bn_ = bank()
                                  for k in range(8):
                                      MM(b_, Wv[:, k, jj * 128:(jj + 1) * 128], srcf(k), k == 0, k == 7,
                                         wr + rsrc(k), bn_)
                                  xr = ["x32_%d_%d" % (j, t) for t in range(NT)]
                                  STT(xT32[:, j, :], xT32[:, j, :], ALPHA, b_, ALU.mult, ALU.add, bn_ + xr, xr)
                          return compute
                      return mk

                  mk = resid_proj(8, mT, R_M)
                  for jq in range(2):
                      tasks.append(Task([(0, [128, 8, 512], "w_o", l, jq * 512, (jq + 1) * 512)], mk(jq)))
                  run_tasks()
                  check('D')
                  p.tag = 'LN1'
                  layer_norm_phase(l, V_LN1G, V_LN1B)
                  check('LN1')

                  p.tag = 'EF'
                  for fp in range(NFF // 2):
                      def up_compute(W, wr, fp=fp):
                          Wa = W[:, 0:2048].rearrange("p (k n) -> p k n", n=256)
                          Wb = W[:, 2048:4096].rearrange("p (k n) -> p k n", n=256)
                          for ff in range(2):
                              f = fp * 2 + ff
                              ba, ban = bank()
                              bb, bbn = bank()
                              for k in range(8):
                                  MM(ba, Wa[:, k, ff * 128:(ff + 1) * 128], xTb[:, k, :], k == 0, k == 7,
                                     wr + ["xb_%d" % k], ban)
                              for k in range(8):
                                  MM(bb, Wb[:, k, ff * 128:(ff + 1) * 128], xTb[:, k, :], k == 0, k == 7,
                                     wr + ["xb_%d" % k], bbn)
                              acc, accn = tmpb()
                              w0 = vcol(l, V_FCW + 0 * NFF + f)
                              w1 = vcol(l, V_FCW + 1 * NFF + f)
                              w2 = vcol(l, V_FCW + 2 * NFF + f)
                              fr = ["fft%d" % l]
                              TS(acc[:], ba, w2, vcol(l, V_FCB + f), ALU.mult, ALU.add, ban + ["vecs"], accn)
                              STT(acc[:, 1:512], ba[:, 0:511], w1, acc[:, 1:512], ALU.mult, ALU.add, ban + accn, accn)
                              STT(acc[:, 2:512], ba[:, 0:510], w0, acc[:, 2:512], ALU.mult, ALU.add, ban + accn, accn)
                              STT(acc[:, 0:1], fft[:, l, f, 1:2], w1, acc[:, 0:1], ALU.mult, ALU.add, fr + accn, accn)
                              STT(acc[:, 0:2], fft[:, l, f, 0:2], w0, acc[:, 0:2], ALU.mult, ALU.add, fr + accn, accn)
                              if pos < st_per_seq - 1:
                                  CP(fft[:, l, f, :], ba[:, 510:512], ban, fr)
                              u, un = tmpb()
                              ACT(u[:], acc[:], AF.Gelu_apprx_tanh, accn, un)
                              TTo(gT(f), u[:], bb, ALU.mult, un + bbn, R_G(f))

                      tasks.append(Task([
                          (0, [128, 8, 256], "w_up", l, fp * 256, (fp + 1) * 256),
                          (2048, [128, 8, 256], "w_up", l, 2816 + fp * 256, 2816 + (fp + 1) * 256),
                      ], up_compute))

                  for j in range(8):
                      def dn_compute(W, wr, j=j):
                          Wv = W[:, 0:NFF * 128].rearrange("p (f n) -> p f n", n=128)
                          b_, bn_ = bank()
                          for f in range(NFF):
                              MM(b_, Wv[:, f, :], gT(f), f == 0, f == NFF - 1, wr + R_G(f), bn_)
                          xr = ["x32_%d_%d" % (j, t) for t in range(NT)]
                          STT(xT32[:, j, :], xT32[:, j, :], ALPHA, b_, ALU.mult, ALU.add, bn_ + xr, xr)

                      tasks.append(Task([(0, [128, NFF, 128], "w_down", l, j * 128, (j + 1) * 128)], dn_compute))
                  run_tasks()
                  check('F')
                  p.tag = 'LN2'
                  layer_norm_phase(l, V_LN2G, V_LN2B, final_tok0=tok0 if last else None)
        except _Stop:
            pass
        p.emit()
    build_program.last_prog = p
    build_program.plan = plan_out
    return nc


def build(L=NL, NST=8, st_per_seq=4, stop=None):
    build_program(L, NST, st_per_seq, stop=stop, plan=None)
    return build_program(L, NST, st_per_seq, stop=stop, plan=build_program.plan)


def _chunks(v):
    return np.ascontiguousarray(np.asarray(v, np.float32).reshape(-1, 128).T)


def host_layout(inp, L=NL):
    vecs = np.zeros((128, NL * NV), np.float32)
    for l in range(L):
        o = l * NV
        vecs[:, o + V_BIN:o + V_BIN + 36] = _chunks(inp["b_in"][l])
        vecs[:, o + V_CONVB:o + V_CONVB + 4] = _chunks(inp["conv_b"][l])
        vecs[:, o + V_CLNG:o + V_CLNG + 4] = _chunks(inp["conv_ln_g"][l])
        vecs[:, o + V_CLNB:o + V_CLNB + 4] = _chunks(inp["conv_ln_b"][l])
        vecs[:, o + V_LN1G:o + V_LN1G + 8] = _chunks(inp["ln1_g"][l])
        vecs[:, o + V_LN1B:o + V_LN1B + 8] = _chunks(inp["ln1_b"][l])
        vecs[:, o + V_LN2G:o + V_LN2G + 8] = _chunks(inp["ln2_g"][l])
        vecs[:, o + V_LN2B:o + V_LN2B + 8] = _chunks(inp["ln2_b"][l])
        vecs[:, o + V_FCB:o + V_FCB + NFF] = _chunks(inp["ffn_conv_b"][l])
        for k in range(3):
            vecs[:, o + V_FCW + k * NFF:o + V_FCW + (k + 1) * NFF] = _chunks(inp["ffn_conv_w"][l][k])
        cw = np.asarray(inp["conv_w"][l], np.float32)
        for j in range(4):
            vecs[:, o + V_CW + j * 31:o + V_CW + (j + 1) * 31] = cw[:, j * 128:(j + 1) * 128].T
    kj = np.arange(128)[:, None, None]
    pp = np.arange(5)[None, :, None]
    qi = np.arange(128)[None, None, :]
    rel = 128 * (4 - pp) + qi - kj
    idx = np.clip(rel, -63, 256) + 63
    qc_minus_kc = 2 * (4 - pp) + (qi >= 64).astype(np.int64) - (kj >= 64).astype(np.int64)
    valid = (qc_minus_kc >= 0) & (qc_minus_kc <= 8)
    rb = np.asarray(inp["rel_bias"], np.float32)
    bias = np.empty((NL, 128, 8, 5, 128), np.float32)
    for l in range(NL):
        g = rb[l][:, idx]
        g = np.where(valid[None], g, np.float32(NEG))
        bias[l] = g.transpose(1, 0, 2, 3)
    bias = bias.reshape(NL, 128, 8 * 640)
    lnf = np.stack([np.broadcast_to(np.asarray(inp["ln2_g"][L - 1], np.float32), (128, D)),
                    np.broadcast_to(np.asarray(inp["ln2_b"][L - 1], np.float32), (128, D))])
    shared = {
        "w_in": np.ascontiguousarray(inp["w_in"], np.float32),
        "w_att_out": np.ascontiguousarray(inp["w_att_out"], np.float32),
        "w_conv_out": np.ascontiguousarray(inp["w_conv_out"], np.float32),
        "w_o": np.ascontiguousarray(inp["w_o"], np.float32),
        "w_up": np.ascontiguousarray(inp["w_up"], np.float32),
        "w_down": np.ascontiguousarray(inp["w_down"], np.float32),
        "vecs": vecs,
        "biasT": bias,
        "lnf": np.ascontiguousarray(lnf),
        "ident": np.eye(128, dtype=np.float32),
    }
    return shared


def kernel(**inputs):
    inp = {k: np.asarray(v) for k, v in inputs.items()}
    x = np.ascontiguousarray(inp["x"], np.float32)
    shared = host_layout(inp)
    n = 8
    nc = build(L=NL, NST=8, st_per_seq=4)
    in_maps = []
    for c in range(n):
        m = dict(shared)
        m["x"] = x[2 * c:2 * c + 2].reshape(2 * SEQ, D)
        in_maps.append(m)
    res = run_bass_kernel_spmd(nc, in_maps, core_ids=list(range(n)))
    out = np.concatenate([np.asarray(r["y"], np.float32).reshape(2, SEQ, D) for r in res.results], axis=0)
    return out
```

```python
import contextlib
import numpy as np
import concourse.bass as bass
import concourse.mybir as mybir
from concourse.bass_utils import run_bass_kernel_spmd

F32 = mybir.dt.float32
BF16 = mybir.dt.bfloat16
AF = mybir.ActivationFunctionType
ALU = mybir.AluOpType

ENGS = ("pe", "act", "dve", "pool", "sp")


class Prog:
    N_DMA_SEMS = 24

    def __init__(self, nc, es):
        self.nc = nc
        self.ops = {e: [] for e in ENGS}
        self.cnt = {e: 0 for e in ENGS}
        self.sem = {}
        for e in ("pe", "act", "dve", "pool"):
            self.sem[e] = es.enter_context(nc.semaphore("s_" + e))
        self.dma_cnt = []
        for i in range(self.N_DMA_SEMS):
            self.sem[("dma", i)] = es.enter_context(nc.semaphore("s_dma%d" % i))
            self.dma_cnt.append(0)
        self.next_dma = [0, 0]
        self.waited = {e: {} for e in ENGS}
        self.res = {}
        self.final_waits = {}
        self.tag = ""
        self.tags = {e: [] for e in ENGS}

    def _deps(self, reads, writes, eng=None):
        deps = {}

        def add(k, v):
            if deps.get(k, 0) < v:
                deps[k] = v

        for r in reads:
            st = self.res.get(r)
            if st is not None and st["w"] is not None:
                add(*st["w"])
            if st is not None and r.startswith("ps"):
                for k, v in st["r"].items():
                    if k != eng:
                        add(k, v)
        for w in writes:
            st = self.res.get(w)
            if st is not None:
                if st["w"] is not None:
                    add(*st["w"])
                for k, v in st["r"].items():
                    add(k, v)
        return deps

    def _waits(self, eng, deps):
        waits = []
        for k, v in deps.items():
            if k == eng and eng == "pe":
                continue
            if self.waited[eng].get(k, 0) >= v:
                continue
            self.waited[eng][k] = v
            waits.append((k, v))
        return waits

    def _record(self, key, val, reads, writes):
        for r in reads:
            st = self.res.setdefault(r, {"w": None, "r": {}})
            if st["r"].get(key, 0) < val:
                st["r"][key] = val
        for w in writes:
            self.res[w] = {"w": (key, val), "r": {}}

    def op(self, eng, fn, reads=(), writes=()):
        waits = self._waits(eng, self._deps(reads, writes, eng))
        self.cnt[eng] += 1
        val = self.cnt[eng]
        self.ops[eng].append((waits, fn, (eng, 1)))
        self.tags[eng].append(self.tag)
        self._record(eng, val, reads, writes)

    def dma(self, queue, out_ap, in_ap, reads=(), writes=(), final=False):
        half = self.N_DMA_SEMS // 2
        qi = 0 if queue == "pool" else 1
        i = qi * half + self.next_dma[qi]
        self.next_dma[qi] = (self.next_dma[qi] + 1) % half
        key = ("dma", i)
        deps = self._deps(reads, writes, queue)
        if self.dma_cnt[i] > 0 and deps.get(key, 0) < self.dma_cnt[i]:
            deps[key] = self.dma_cnt[i]
        waits = self._waits(queue, deps)
        self.dma_cnt[i] += 16
        val = self.dma_cnt[i]

        def fn(e, out_ap=out_ap, in_ap=in_ap, queue=queue):
            if queue == "pool":
                return e.dma_start(out=out_ap, in_=in_ap, max_dma_last_dim=4096)
            return e.dma_start(out=out_ap, in_=in_ap)

        self.ops[queue].append((waits, fn, (key, 16)))
        self._record(key, val, reads, writes)
        if final:
            self.final_waits[key] = val

    def emit(self):
        nc = self.nc
        fw = [(k, v) for k, v in self.final_waits.items()]
        sem = self.sem
        ops = self.ops

        def run(eng_name, e):
            for waits, fn, (k, inc) in ops[eng_name]:
                for wk, wv in waits:
                    e.wait_ge(sem[wk], wv)
                fn(e).then_inc(sem[k], inc)

        with nc.Block() as block:

            @block.tensor
            def _(e):
                run("pe", e)

            @block.scalar
            def _(e):
                run("act", e)

            @block.vector
            def _(e):
                run("dve", e)

            @block.gpsimd
            def _(e):
                run("pool", e)

            @block.sync
            def _(e):
                run("sp", e)
                for k, v in fw:
                    e.wait_ge(sem[k], v)


D = 1024
NL = 4
SEQ = 2048
TT = 512
NT = TT // 128
ALPHA = 8.0 ** 0.25
LN_EPS = 1e-5
NFF = 22
NV = 292
NEG = -30000.0
GELU_C = 0.7978845608028654

V_BIN = 0
V_CONVB = 36
V_CLNG = 40
V_CLNB = 44
V_LN1G = 48
V_LN1B = 56
V_LN2G = 64
V_LN2B = 72
V_FCB = 80
V_FCW = 102
V_CW = 168


class _Stop(Exception):
    pass


NTPL = 34


def build_program(L=NL, NST=8, st_per_seq=4, stop=None, plan=None):
    nc = bass.Bass("TRN2", target_bir_lowering=False)
    ntok = NST * TT
    x_d = nc.dram_tensor("x", [ntok, D], F32, kind="ExternalInput").ap()
    y_d = nc.dram_tensor("y", [ntok, D], F32, kind="ExternalOutput").ap()
    w_in_d = nc.dram_tensor("w_in", [NL, D, 4608], F32, kind="ExternalInput").ap()
    w_ao_d = nc.dram_tensor("w_att_out", [NL, 512, D], F32, kind="ExternalInput").ap()
    w_co_d = nc.dram_tensor("w_conv_out", [NL, 512, D], F32, kind="ExternalInput").ap()
    w_o_d = nc.dram_tensor("w_o", [NL, D, D], F32, kind="ExternalInput").ap()
    w_up_d = nc.dram_tensor("w_up", [NL, D, 5632], F32, kind="ExternalInput").ap()
    w_dn_d = nc.dram_tensor("w_down", [NL, 2816, D], F32, kind="ExternalInput").ap()
    vecs_d = nc.dram_tensor("vecs", [128, NL * NV], F32, kind="ExternalInput").ap()
    bias_d = nc.dram_tensor("biasT", [NL, 128, 8 * 640], F32, kind="ExternalInput").ap()
    lnf_d = nc.dram_tensor("lnf", [2, 128, D], F32, kind="ExternalInput").ap()
    id_d = nc.dram_tensor("ident", [128, 128], F32, kind="ExternalInput").ap()
    scr_d = nc.dram_tensor("wscr", [NL * NTPL, 128, 4096], BF16).ap()
    dram = {"w_in": w_in_d, "w_att_out": w_ao_d, "w_conv_out": w_co_d, "w_o": w_o_d, "w_up": w_up_d,
            "w_down": w_dn_d}
    plan_out = []

    with contextlib.ExitStack() as es:
        p = Prog(nc, es)

        def sb(name, shape, dt):
            return es.enter_context(nc.sbuf_tensor("sb_" + name, shape, dt))

        xT32 = sb("xT32", [128, 8, TT], F32)
        xTb = sb("xTb", [128, 8, TT], BF16)
        kTh = sb("kTh", [128, NL, 4, 512], BF16)
        vh = sb("vh", [128, NL * 32, 65], BF16)
        hct = sb("hct", [128, NL, 4, 30], BF16)
        fft = sb("fft", [128, NL, NFF, 2], F32)
        vecs = sb("vecs", [128, NL * NV], F32)
        biasT = sb("biasT", [128, 8, 640], BF16)
        ident = sb("ident", [128, 128], F32)
        lnf = sb("lnf", [128, 2, D], F32)
        epsT = sb("epsT", [128, 1], F32)
        A_Q, A_K, A_C, A_M, A_AT = 0, 4096, 6144, 8192, 12288
        A_HC = 14336
        A_V = A_HC + 4 * 542 + 8
        A_D = A_V + 4 * 520
        A_END = A_D + 31 * 128
        arena = sb("arena", [128, A_END], BF16)

        def units(lo, hi):
            return ["u%d" % i for i in range(lo // 512, (hi - 1) // 512 + 1)]

        def qT(j, hh):
            o = A_Q + hh * 2048 + j * 512
            return arena[:, o:o + 512]

        def kT(j):
            return arena[:, A_K + j * 512:A_K + (j + 1) * 512]

        def cT(j):
            return arena[:, A_C + j * 512:A_C + (j + 1) * 512]

        def mT(j):
            return arena[:, A_M + j * 512:A_M + (j + 1) * 512]

        def aT(j):
            return arena[:, A_AT + j * 512:A_AT + (j + 1) * 512]

        def hc(j):
            return arena[:, A_HC + j * 542:A_HC + (j + 1) * 542]

        def vcur(t):
            return arena[:, A_V + t * 520:A_V + (t + 1) * 520].rearrange("p (h d) -> p h d", d=65)

        def gT(f):
            return arena[:, f * 512:(f + 1) * 512]

        Dg = arena[:, A_D:A_END].rearrange("p (k m) -> p k m", m=128)
        R_Q = lambda j: units(A_Q + j * 512, A_Q + (j + 1) * 512) + units(A_Q + 2048 + j * 512, A_Q + 2048 + (j + 1) * 512)
        R_K = lambda j: units(A_K + j * 512, A_K + (j + 1) * 512)
        R_C = lambda j: units(A_C + j * 512, A_C + (j + 1) * 512)
        R_M = lambda j: units(A_M + j * 512, A_M + (j + 1) * 512)
        R_AT = lambda j: units(A_AT + j * 512, A_AT + (j + 1) * 512)
        R_HC = lambda j: units(A_HC + j * 542, A_HC + (j + 1) * 542)
        R_V = lambda t: units(A_V + t * 520, A_V + (t + 1) * 520)
        R_D = units(A_D, A_END)
        R_G = lambda f: units(f * 512, (f + 1) * 512)
        R_VALL = units(A_V, A_V + 4 * 520)

        NTMP = 4
        tmp = [sb("tmp%d" % i, [128, 512], F32) for i in range(NTMP)]
        id8 = sb("id8", [128, 128], BF16)
        PT = [sb("PT%d" % i, [128, 640], BF16) for i in range(4)]
        h2 = sb("h2", [128, 4, TT], F32)
        xn = [sb("xn%d" % i, [128, D], F32) for i in range(2)]
        atok = [sb("atok%d" % i, [128, 512], F32) for i in range(2)]
        st6 = sb("st6", [128, 4, 2, 6], F32)
        mv = sb("mv", [128, 4, 2], F32)
        rs = sb("rs", [128, 4, 3], F32)
        rc = sb("rc", [128, 8], F32)
        NSLOT = 4
        wsl = [sb("wsl%d" % i, [128, 4096], BF16) for i in range(NSLOT)]
        ps = es.enter_context(nc.psum_tensor("ps", [128, 8, 512], F32))

        cnt = {"tmp": 0, "bank": 0, "pair": 0, "lg": 0, "pt": 0, "xn": 0, "atok": 0, "slot": 0}

        def rot(key, n):
            i = cnt[key] % n
            cnt[key] = (i + 1) % n
            return i

        ring = {"banks": 6}

        def bank():
            b = rot("bank", ring["banks"])
            return ps[:, b, :], ["ps%d" % b]

        def pair():
            b = 2 * rot("pair", ring["banks"] // 2)
            return ps[:, b:b + 2, :], ["ps%d" % b, "ps%d" % (b + 1)]

        def tmpb():
            i = rot("tmp", NTMP)
            return tmp[i], ["tmp%d" % i]

        def MM(out, lhsT, rhs, start, stop, reads, writes):
            p.op("pe", lambda e: e.matmul(out, lhsT, rhs, start=start, stop=stop), reads, writes)

        def TR(out, in_, reads, writes):
            p.op("pe", lambda e: e.transpose(out, in_, ident[:]), list(reads) + ["ident"], writes)

        def ACT(out, in_, func, reads, writes, bias=None, scale=None):
            kw = {}
            if bias is not None:
                kw["bias"] = bias
            if scale is not None:
                kw["scale"] = scale
            p.op("act", lambda e: e.activation(out=out, in_=in_, func=func, **kw), reads, writes)

        def TS(out, in0, s1, s2, op0, op1, reads, writes, eng="dve"):
            if s2 is None:
                p.op(eng, lambda e: e.tensor_scalar(out=out, in0=in0, scalar1=s1, scalar2=None, op0=op0),
                     reads, writes)
            else:
                p.op(eng, lambda e: e.tensor_scalar(out=out, in0=in0, scalar1=s1, scalar2=s2, op0=op0, op1=op1),
                     reads, writes)

        def STT(out, in0, scalar, in1, op0, op1, reads, writes):
            p.op("dve", lambda e: e.scalar_tensor_tensor(out=out, in0=in0, scalar=scalar, in1=in1, op0=op0, op1=op1),
                 reads, writes)

        def TTo(out, in0, in1, op, reads, writes, eng="dve"):
            p.op(eng, lambda e: e.tensor_tensor(out=out, in0=in0, in1=in1, op=op), reads, writes)

        def CP(out, in_, reads, writes, eng="dve"):
            p.op(eng, lambda e: e.tensor_copy(out=out, in_=in_), reads, writes)

        def MS(ap, val, writes, eng="dve"):
            p.op(eng, lambda e: e.memset(ap, val), (), writes)

        def vcol(l, c):
            return vecs[:, l * NV + c:l * NV + c + 1]

        pending = []

        class Task:
            def __init__(self, loads, compute):
                self.loads = loads
                self.compute = compute

        tasks = []
        ntask = [0]

        gstate = {"issued": 0, "done": 0, "st": 0, "l": 0, "ti": 0}

        def src_of(spec):
            off, shape, name, l_, c0, c1 = spec
            return dram[name][l_][:, c0:c1].rearrange("(k p) n -> p k n", p=128)

        def issue_entry(ent, s):
            used = 0
            for (off, shape, name, l_, c0, c1) in ent["specs"]:
                n = shape[1] * shape[2]
                used = max(used, off + n)
            idx = ent["l"] * NTPL + ent["ti"]
            if ent["st"] == 0:
                for spec in ent["specs"]:
                    off, shape = spec[0], spec[1]
                    n = shape[1] * shape[2]
                    dst = wsl[s][:, off:off + n].rearrange("p (a b) -> p a b", b=shape[2])
                    p.dma("pool", dst, src_of(spec), reads=(), writes=["w%d" % s])
                p.dma("sp", scr_d[idx][:, 0:used], wsl[s][:, 0:used], reads=["w%d" % s], writes=["scr%d" % idx])
            else:
                p.dma("sp", wsl[s][:, 0:used], scr_d[idx][:, 0:used], reads=["scr%d" % idx], writes=["w%d" % s])

        def run_tasks():
            for tk in tasks:
                gi = gstate["done"]
                if plan is None:
                    ent = {"specs": tk.loads, "st": gstate["st"], "l": gstate["l"], "ti": gstate["ti"]}
                    plan_out.append(ent)
                    issue_entry(ent, gi % NSLOT)
                else:
                    while gstate["issued"] < min(len(plan), gi + NSLOT):
                        k = gstate["issued"]
                        issue_entry(plan[k], k % NSLOT)
                        gstate["issued"] += 1
                s_ = gi % NSLOT
                tk.compute(wsl[s_], ["w%d" % s_])
                gstate["done"] += 1
                gstate["ti"] += 1
                ntask[0] += 1
                if stop == "T%d" % ntask[0]:
                    raise _Stop()
            del tasks[:]


        p.dma("sp", vecs[:], vecs_d, writes=["vecs"])
        p.dma("sp", ident[:], id_d, writes=["ident"])
        p.dma("sp", lnf[:], lnf_d.rearrange("a p n -> p a n"), writes=["lnf"])
        MS(epsT[:], LN_EPS, ["eps"])
        TS(id8[:], ident[:], 8.0, None, ALU.mult, None, ["ident"], ["id8"])
        MS(vh[:, :, 64:65], 1.0, ["vh%d" % l for l in range(NL)])

        def layer_norm_phase(l, gcol, bcol, final_tok0=None):
            st_ = {}

            def I(t):
                cols = slice(t * 128, (t + 1) * 128)
                pa, pan = pair()
                pav = pa.rearrange("p a n -> p (a n)")
                for j in range(8):
                    TR(pav[:, j * 128:(j + 1) * 128], xT32[:, j, cols], ["x32_%d_%d" % (j, t)], pan)
                st_[t] = (pav, pan)

            def S(t):
                pav, pan = st_[t]
                p.op("dve", lambda e: e.bn_stats(out=st6[:, t, 0, :], in_=pav[:, 0:512]), pan, ["st6a%d" % t])
                p.op("dve", lambda e: e.bn_stats(out=st6[:, t, 1, :], in_=pav[:, 512:1024]), pan, ["st6b%d" % t])
                p.op("dve", lambda e: e.bn_aggr(out=mv[:, t, :], in_=st6[:, t, :, :].rearrange("p a s -> p (a s)")),
                     ["st6a%d" % t, "st6b%d" % t], ["mv%d" % t])
                ACT(rs[:, t, 0:1], mv[:, t, 1:2], AF.Sqrt, ["mv%d" % t, "eps"], ["rsa%d" % t],
                    bias=epsT[:, 0:1], scale=1.0)
                p.op("dve", lambda e: e.reciprocal(out=rs[:, t, 1:2], in_=rs[:, t, 0:1]), ["rsa%d" % t], ["rsb%d" % t])
                xi = rot("xn", 2)
                TS(rs[:, t, 2:3], mv[:, t, 0:1], rs[:, t, 1:2], -1.0, ALU.mult, ALU.mult,
                   ["mv%d" % t, "rsb%d" % t], ["rsc%d" % t])
                ACT(xn[xi][:], pav, AF.Identity, pan + ["rsb%d" % t, "rsc%d" % t], ["xn%d" % xi],
                    bias=rs[:, t, 2:3], scale=rs[:, t, 1:2])
                st_[t] = xi

            def B(t):
                xi = st_[t]
                pb, pbn = pair()
                pbv = pb.rearrange("p a n -> p (a n)")
                for j in range(8):
                    TR(pbv[:, j * 128:(j + 1) * 128], xn[xi][:, j * 128:(j + 1) * 128], ["xn%d" % xi],
                       pbn[j // 4:j // 4 + 1])
                st_[t] = (pbv, pbn)

            def E(t):
                cols = slice(t * 128, (t + 1) * 128)
                pbv, pbn = st_[t]
                for j in range(4):
                    TS(xT32[:, j, cols], pbv[:, j * 128:(j + 1) * 128], vcol(l, gcol + j), vcol(l, bcol + j),
                       ALU.mult, ALU.add, pbn[0:1] + ["vecs"], ["x32_%d_%d" % (j, t)])
                for j in range(4, 8):
                    ACT(xT32[:, j, cols], pbv[:, j * 128:(j + 1) * 128], AF.Identity, pbn[1:2] + ["vecs"],
                        ["x32_%d_%d" % (j, t)], bias=vcol(l, bcol + j), scale=vcol(l, gcol + j))
                CP(xTb[:, 0:4, cols], xT32[:, 0:4, cols], ["x32_%d_%d" % (j, t) for j in range(4)], R_XB[0:4])
                ACT(xTb[:, 4:8, cols], xT32[:, 4:8, cols], AF.Copy, ["x32_%d_%d" % (j, t) for j in range(4, 8)],
                    R_XB[4:8])

            def F(t):
                xi = st_[t]
                r0 = final_tok0 + t * 128
                TTo(xn[xi][:], xn[xi][:], lnf[:, 0, :], ALU.mult, ["xn%d" % xi, "lnf"], ["xn%d" % xi])
                TTo(xn[xi][:], xn[xi][:], lnf[:, 1, :], ALU.add, ["xn%d" % xi, "lnf"], ["xn%d" % xi])
                p.dma("sp", y_d[r0:r0 + 128, :], xn[xi][:], reads=["xn%d" % xi], final=True)

            ring["banks"] = 8
            if final_tok0 is None:
                for f, t in [(I, 0), (I, 1), (I, 2), (I, 3), (S, 0), (B, 0), (S, 1), (B, 1), (E, 0), (S, 2),
                             (B, 2), (E, 1), (S, 3), (B, 3), (E, 2), (E, 3)]:
                    f(t)
            else:
                for f, t in [(I, 0), (I, 1), (I, 2), (I, 3), (S, 0), (F, 0), (S, 1), (F, 1), (S, 2), (F, 2),
                             (S, 3), (F, 3)]:
                    f(t)
            ring["banks"] = 6

        R_XB = ["xb_%d" % j for j in range(8)]

        def check(name):
            if stop == name:
                raise _Stop()

        try:
          for st in range(NST):
              pos = st % st_per_seq
              B0 = pos * NT
              tok0 = st * TT
              p.tag = 'X'
              for t in range(NT):
                  xi = rot("xn", 2)
                  p.dma("sp", xn[xi][:], x_d[tok0 + t * 128:tok0 + (t + 1) * 128, :], writes=["xn%d" % xi])
                  pb, pbn = pair()
                  pbv = pb.rearrange("p a n -> p (a n)")
                  for j in range(8):
                      TR(pbv[:, j * 128:(j + 1) * 128], xn[xi][:, j * 128:(j + 1) * 128], ["xn%d" % xi], pbn)
                  cols = slice(t * 128, (t + 1) * 128)
                  CP(xT32[:, :, cols], pb.rearrange("p a (b n) -> p (a b) n", n=128), pbn,
                     ["x32_%d_%d" % (j, t) for j in range(8)])
                  ACT(xTb[:, :, cols], xT32[:, :, cols], AF.Copy, ["x32_%d_%d" % (j, t) for j in range(8)], R_XB)

              check('X')
              for l in range(L):
                  last = (l == L - 1)
                  gstate["st"], gstate["l"], gstate["ti"] = st, l, 0
                  p.dma("pool", biasT[:].rearrange("p h n -> p (h n)"), bias_d[l], writes=["biasT"])
                  if pos == 0:
                      MS(hc(0)[:, 0:30], 0.0, R_HC(0))
                      for j in range(1, 4):
                          MS(hc(j)[:, 0:30], 0.0, R_HC(j))
                      MS(fft[:, l, :, :], 0.0, ["fft%d" % l])
                  else:
                      for j in range(4):
                          CP(hc(j)[:, 0:30], hct[:, l, j, :], ["hct%d" % l], R_HC(j))
                  for t in range(NT):
                      MS(vcur(t)[:, :, 64:65], 1.0, R_V(t))
                  check('S')

                  p.tag = 'A'
                  def proj_fm(dstf, rdst, bcol0):
                      def compute(W, wr):
                          Wv = W[:, 0:4096].rearrange("p (k n) -> p k n", n=512)
                          for j in range(4):
                              b_, bn_ = bank()
                              for k in range(8):
                                  MM(b_, Wv[:, k, j * 128:(j + 1) * 128], xTb[:, k, :], k == 0, k == 7,
                                     wr + ["xb_%d" % k], bn_)
                              ACT(dstf(j), b_, AF.Identity, bn_ + ["vecs"], rdst(j), bias=vcol(l, bcol0 + j))
                      return compute

                  def q_compute(W, wr):
                      Wv = W[:, 0:4096].rearrange("p (k n) -> p k n", n=512)
                      for j in range(4):
                          b_, bn_ = bank()
                          for k in range(8):
                              MM(b_, Wv[:, k, j * 128:(j + 1) * 128], xTb[:, k, :], k == 0, k == 7,
                                 wr + ["xb_%d" % k], bn_)
                          for hh in range(2):
                              r = slice(hh * 64, hh * 64 + 64)
                              ACT(qT(j, hh)[r, :], b_[r, :], AF.Identity, bn_ + ["vecs"], R_Q(j),
                                  bias=vecs[r, l * NV + V_BIN + j:l * NV + V_BIN + j + 1])

                  MS(arena[64:128, A_Q:A_Q + 2048], 0.0, [u for j in range(4) for u in R_Q(j)])
                  MS(arena[0:64, A_Q + 2048:A_Q + 4096], 0.0, [u for j in range(4) for u in R_Q(j)])
                  tasks.append(Task([(0, [128, 8, 512], "w_in", l, 0, 512)], q_compute))
                  tasks.append(Task([(0, [128, 8, 512], "w_in", l, 512, 1024)], proj_fm(kT, R_K, V_BIN + 4)))

                  def v_compute(W, wr):
                      Wv = W[:, 0:4096].rearrange("p (k n) -> p k n", n=512)
                      for t in range(NT):
                          b_, bn_ = bank()
                          for k in range(8):
                              MM(b_, xTb[:, k, t * 128:(t + 1) * 128], Wv[:, k, :], k == 0, k == 7,
                                 wr + ["xb_%d" % k], bn_)
                          CP(vcur(t)[:, :, 0:64], b_.rearrange("p (h d) -> p h d", d=64), bn_, R_V(t))

                  tasks.append(Task([(0, [128, 8, 512], "w_in", l, 1024, 1536)], v_compute))

                  def glu_compute(W, wr):
                      Wa = W[:, 0:2048].rearrange("p (k n) -> p k n", n=256)
                      Wg = W[:, 2048:4096].rearrange("p (k n) -> p k n", n=256)
                      return Wa, Wg

                  for half in range(2):
                      def c_compute(W, wr, half=half):
                          Wa = W[:, 0:2048].rearrange("p (k n) -> p k n", n=256)
                          Wg = W[:, 2048:4096].rearrange("p (k n) -> p k n", n=256)
                          for jj in range(2):
                              j = half * 2 + jj
                              ba, ban = bank()
                              bg, bgn = bank()
                              for k in range(8):
                                  MM(ba, Wa[:, k, jj * 128:(jj + 1) * 128], xTb[:, k, :], k == 0, k == 7,
                                     wr + ["xb_%d" % k], ban)
                              for k in range(8):
                                  MM(bg, Wg[:, k, jj * 128:(jj + 1) * 128], xTb[:, k, :], k == 0, k == 7,
                                     wr + ["xb_%d" % k], bgn)
                              tg, tgn = tmpb()
                              ACT(tg[:], bg, AF.Sigmoid, bgn + ["vecs"], tgn, bias=vcol(l, V_BIN + 16 + j))
                              STT(hc(j)[:, 30:542], ba, vcol(l, V_BIN + 12 + j), tg[:], ALU.add, ALU.mult,
                                  ban + tgn + ["vecs"], R_HC(j))

                      c0 = 1536 + half * 256
                      tasks.append(Task([(0, [128, 8, 256], "w_in", l, c0, c0 + 256),
                                         (2048, [128, 8, 256], "w_in", l, c0 + 512, c0 + 768)], c_compute))
                  run_tasks()

                  check('A')
                  if pos < st_per_seq - 1:
                      for j in range(4):
                          CP(hct[:, l, j, :], hc(j)[:, 512:542], R_HC(j), ["hct%d" % l])

                  def conv_steps():
                      NDR = 7
                      for half in range(2):
                          accs = [(ps[:, 4 + jj, :], ["ps%d" % (4 + jj)]) for jj in range(2)]
                          for k in range(31):
                              sl = k % NDR
                              for jj in range(2):
                                  j = half * 2 + jj
                                  dt_ = Dg[:, sl * 4 + jj, :]
                                  TS(dt_, ident[:], vcol(l, V_CW + j * 31 + k), None, ALU.mult, None,
                                     ["ident", "vecs"], ["Dg%d_%d" % (sl, jj)])
                                  MM(accs[jj][0], dt_, hc(j)[:, k:k + 512], k == 0, k == 30,
                                     ["Dg%d_%d" % (sl, jj)] + R_HC(j), accs[jj][1])
                              yield
                          for jj in range(2):
                              j = half * 2 + jj
                              ACT(h2[:, j, :], accs[jj][0], AF.Identity, accs[jj][1] + ["vecs"], ["h2_%d" % j],
                                  bias=vcol(l, V_CONVB + j))
                          yield

                  conv_it = conv_steps()

                  def conv_step(n=1):
                      for _ in range(n):
                          try:
                              next(conv_it)
                          except StopIteration:
                              return

                  p.tag = 'B2'
                  ring["banks"] = 4
                  pend_tail = [None]
                  for g in range(NT):
                      gb = B0 + g
                      p_lo = max(0, 4 - gb)
                      c_lo = p_lo * 128
                      qcols = slice(g * 128, (g + 1) * 128)
                      O, On = ps[:, 6:8, :], ["ps6", "ps7"]
                      Ov = [O[:, hh, 0:260].rearrange("p (h d) -> p h d", d=65) for hh in range(2)]

                      def pv(h, pti):
                          for pp in range(p_lo, 5):
                              rel = gb - 4 + pp - B0
                              if rel < 0:
                                  vsrc = vh[:, (l * 4 + rel + 4) * 8 + h, :]
                                  vr = ["vh%d" % l]
                              else:
                                  vsrc = vcur(rel)[:, h, :]
                                  vr = R_V(rel)
                              MM(Ov[h // 4][:, h % 4, :], PT[pti][:, pp * 128:(pp + 1) * 128], vsrc,
                                 pp == p_lo, pp == 4, ["PT%d" % pti] + vr, On)

                      def qk(h):
                          hp, hh = h // 2, h % 2
                          S, Sn = pair()
                          Sv = S.rearrange("p a n -> p (a n)")
                          for pp in range(p_lo, 5):
                              rel = gb - 4 + pp - B0
                              if rel < 0:
                                  ksrc = kTh[:, l, hp, (rel + 4) * 128:(rel + 5) * 128]
                                  kr = ["kTh%d" % l]
                              else:
                                  ksrc = kT(hp)[:, rel * 128:(rel + 1) * 128]
                                  kr = R_K(hp)
                              blk = Sv[:, pp * 128:(pp + 1) * 128]
                              MM(blk, ksrc, qT(hp, hh)[:, qcols], True, False, kr + R_Q(hp), Sn)
                              MM(blk, id8[:], biasT[:, h, pp * 128:(pp + 1) * 128], False, True,
                                 ["id8", "biasT"], Sn)
                          pti = rot("pt", 4)
                          ACT(PT[pti][:, c_lo:640], Sv[:, c_lo:640], AF.Exp, Sn, ["PT%d" % pti], scale=0.125)
                          return (h, pti)

                      SKEW = 2
                      inflight = [qk(h) for h in range(SKEW)]
                      for h in range(8):
                          if h + SKEW < 8:
                              inflight.append(qk(h + SKEW))
                          conv_step(2)
                          if h == 1 and pend_tail[0] is not None:
                              pend_tail[0]()
                              pend_tail[0] = None
                          pv(*inflight.pop(0))
                      ai = rot("atok", 2)
                      for hh in range(2):
                          p.op("dve", lambda e, hh=hh: e.reciprocal(out=rc[:, hh * 4:(hh + 1) * 4], in_=Ov[hh][:, :, 64]),
                               On, ["rc%d" % hh])
                          TTo(atok[ai][:, hh * 256:(hh + 1) * 256].rearrange("p (h d) -> p h d", d=64),
                              Ov[hh][:, :, 0:64],
                              rc[:, hh * 4:(hh + 1) * 4].unsqueeze(2).broadcast_to([128, 4, 64]),
                              ALU.mult, On + ["rc%d" % hh], ["atok%d_%d" % (ai, hh)])
                      def tail(ai=ai, qcols=qcols):
                          b_, bn_ = bank()
                          for c in range(4):
                              TR(b_[:, c * 128:(c + 1) * 128], atok[ai][:, c * 128:(c + 1) * 128],
                                 ["atok%d_0" % ai, "atok%d_1" % ai], bn_)
                          for c in range(4):
                              ACT(aT(c)[:, qcols], b_[:, c * 128:(c + 1) * 128], AF.Identity, bn_ + ["vecs"],
                                  R_AT(c), bias=vcol(l, V_BIN + 8 + c))

                      pend_tail[0] = tail
                  pend_tail[0]()
                  conv_step(100)
                  ring["banks"] = 6
                  check('B2')
                  p.tag = 'B1'
                  cst = {}

                  def cI(t):
                      cols = slice(t * 128, (t + 1) * 128)
                      b_, bn_ = bank()
                      for j in range(4):
                          TR(b_[:, j * 128:(j + 1) * 128], h2[:, j, cols], ["h2_%d" % j], bn_)
                      cst[t] = (b_, bn_)

                  def cS(t):
                      b_, bn_ = cst[t]
                      p.op("dve", lambda e: e.bn_stats(out=st6[:, t, 0, :], in_=b_), bn_, ["st6a%d" % t])
                      p.op("dve", lambda e: e.bn_aggr(out=mv[:, t, :], in_=st6[:, t, 0, :]), ["st6a%d" % t], ["mv%d" % t])
                      ACT(rs[:, t, 0:1], mv[:, t, 1:2], AF.Sqrt, ["mv%d" % t, "eps"], ["rsa%d" % t],
                          bias=epsT[:, 0:1], scale=1.0)
                      p.op("dve", lambda e: e.reciprocal(out=rs[:, t, 1:2], in_=rs[:, t, 0:1]), ["rsa%d" % t], ["rsb%d" % t])
                      tn, tnn = tmpb()
                      TS(tn[:], b_, mv[:, t, 0:1], rs[:, t, 1:2], ALU.subtract, ALU.mult,
                         bn_ + ["mv%d" % t, "rsb%d" % t], tnn)
                      cst[t] = (tn, tnn)

                  def cB(t):
                      tn, tnn = cst[t]
                      b2, b2n = bank()
                      for j in range(4):
                          TR(b2[:, j * 128:(j + 1) * 128], tn[:, j * 128:(j + 1) * 128], tnn, b2n)
                      cst[t] = (b2, b2n)

                  def cE(t):
                      cols = slice(t * 128, (t + 1) * 128)
                      b2, b2n = cst[t]
                      for j in range(4):
                          ACT(cT(j)[:, cols], b2[:, j * 128:(j + 1) * 128], AF.Silu, b2n + ["vecs"], R_C(j),
                              bias=vcol(l, V_CLNB + j), scale=vcol(l, V_CLNG + j))

                  for f, t in [(cI, 0), (cI, 1), (cI, 2), (cI, 3), (cS, 0), (cB, 0), (cS, 1), (cB, 1), (cE, 0),
                               (cS, 2), (cB, 2), (cE, 1), (cS, 3), (cB, 3), (cE, 2), (cE, 3)]:
                      f(t)

                  if pos < st_per_seq - 1:
                      for j in range(4):
                          CP(kTh[:, l, j, :], kT(j), R_K(j), ["kTh%d" % l])
                      for t in range(NT):
                          CP(vh[:, (l * 4 + t) * 8:(l * 4 + t + 1) * 8, 0:64], vcur(t)[:, :, 0:64], R_V(t), ["vh%d" % l])

                  p.tag = 'CD'
                  for j in range(8):
                      def m_compute(W, wr, j=j):
                          Wao = W[:, 0:512].rearrange("p (k n) -> p k n", n=128)
                          Wco = W[:, 512:1024].rearrange("p (k n) -> p k n", n=128)
                          Wga = W[:, 1024:2048].rearrange("p (k n) -> p k n", n=128)
                          Wgc = W[:, 2048:3072].rearrange("p (k n) -> p k n", n=128)
                          ya, yan = bank()
                          yc, ycn = bank()
                          za, zan = bank()
                          zc, zcn = bank()
                          for k in range(4):
                              MM(ya, Wao[:, k, :], aT(k), k == 0, k == 3, wr + R_AT(k), yan)
                          for k in range(4):
                              MM(yc, Wco[:, k, :], cT(k), k == 0, k == 3, wr + R_C(k), ycn)
                          for k in range(8):
                              MM(za, Wga[:, k, :], xTb[:, k, :], k == 0, k == 7, wr + ["xb_%d" % k], zan)
                          for k in range(8):
                              MM(zc, Wgc[:, k, :], xTb[:, k, :], k == 0, k == 7, wr + ["xb_%d" % k], zcn)
                          s1, s1n = tmpb()
                          s2, s2n = tmpb()
                          ACT(s1[:], za, AF.Sigmoid, zan + ["vecs"], s1n, bias=vcol(l, V_BIN + 20 + j))
                          ACT(s2[:], zc, AF.Sigmoid, zcn + ["vecs"], s2n, bias=vcol(l, V_BIN + 28 + j))
                          TTo(s1[:], ya, s1[:], ALU.mult, yan + s1n, s1n)
                          TTo(s2[:], yc, s2[:], ALU.mult, ycn + s2n, s2n)
                          TTo(mT(j), s1[:], s2[:], ALU.add, s1n + s2n, R_M(j))

                      tasks.append(Task([
                          (0, [128, 4, 128], "w_att_out", l, j * 128, (j + 1) * 128),
                          (512, [128, 4, 128], "w_conv_out", l, j * 128, (j + 1) * 128),
                          (1024, [128, 8, 128], "w_in", l, 2560 + j * 128, 2560 + (j + 1) * 128),
                          (2048, [128, 8, 128], "w_in", l, 3584 + j * 128, 3584 + (j + 1) * 128),
                      ], m_compute))

                  def resid_proj(nk, srcf, rsrc):
                      def mk(jq):
                          def compute(W, wr):
                              Wv = W[:, 0:nk * 512].rearrange("p (k n) -> p k n", n=512) if nk == 8 else None
                              for jj in range(4):
                                  j = jq * 4 + jj
                                  b_, bn_ = bank()
                                  for k in range(8):
                                      MM(b_, Wv[:, k, jj * 128:(jj + 1) * 128], srcf(k), k == 0, k == 7,
                                         wr + rsrc(k), bn_)
                                  xr = ["x32_%d_%d" % (j, t) for t in range(NT)]
                                  STT(xT32[:, j, :], xT32[:, j, :], ALPHA, b_, ALU.mult, ALU.add, bn_ + xr, xr)
                          return compute
                      return mk

                  mk = resid_proj(8, mT, R_M)
                  for jq in range(2):
                      tasks.append(Task([(0, [128, 8, 512], "w_o", l, jq * 512, (jq + 1) * 512)], mk(jq)))
                  run_tasks()
                  check('D')
                  p.tag = 'LN1'
                  layer_norm_phase(l, V_LN1G, V_LN1B)
                  check('LN1')

                  p.tag = 'EF'
                  for fp in range(NFF // 2):
                      def up_compute(W, wr, fp=fp):
                          Wa = W[:, 0:2048].rearrange("p (k n) -> p k n", n=256)
                          Wb = W[:, 2048:4096].rearrange("p (k n) -> p k n", n=256)
                          for ff in range(2):
                              f = fp * 2 + ff
                              ba, ban = bank()
                              bb, bbn = bank()
                              for k in range(8):
                                  MM(ba, Wa[:, k, ff * 128:(ff + 1) * 128], xTb[:, k, :], k == 0, k == 7,
                                     wr + ["xb_%d" % k], ban)
                              for k in range(8):
                                  MM(bb, Wb[:, k, ff * 128:(ff + 1) * 128], xTb[:, k, :], k == 0, k == 7,
                                     wr + ["xb_%d" % k], bbn)
                              acc, accn = tmpb()
                              w0 = vcol(l, V_FCW + 0 * NFF + f)
                              w1 = vcol(l, V_FCW + 1 * NFF + f)
                              w2 = vcol(l, V_FCW + 2 * NFF + f)
                              fr = ["fft%d" % l]
                              TS(acc[:], ba, w2, vcol(l, V_FCB + f), ALU.mult, ALU.add, ban + ["vecs"], accn)
                              STT(acc[:, 1:512], ba[:, 0:511], w1, acc[:, 1:512], ALU.mult, ALU.add, ban + accn, accn)
                              STT(acc[:, 2:512], ba[:, 0:510], w0, acc[:, 2:512], ALU.mult, ALU.add, ban + accn, accn)
                              STT(acc[:, 0:1], fft[:, l, f, 1:2], w1, acc[:, 0:1], ALU.mult, ALU.add, fr + accn, accn)
                              STT(acc[:, 0:2], fft[:, l, f, 0:2], w0, acc[:, 0:2], ALU.mult, ALU.add, fr + accn, accn)
                              if pos < st_per_seq - 1:
                                  CP(fft[:, l, f, :], ba[:, 510:512], ban, fr)
                              u, un = tmpb()
                              ACT(u[:], acc[:], AF.Gelu_apprx_tanh, accn, un)
                              TTo(gT(f), u[:], bb, ALU.mult, un + bbn, R_G(f))

                      tasks.append(Task([
                          (0, [128, 8, 256], "w_up", l, fp * 256, (fp + 1) * 256),
                          (2048, [128, 8, 256], "w_up", l, 2816 + fp * 256, 2816 + (fp + 1) * 256),
                      ], up_compute))

                  for j in range(8):
                      def dn_compute(W, wr, j=j):
                          Wv = W[:, 0:NFF * 128].rearrange("p (f n) -> p f n", n=128)
                          b_, bn_ = bank()
                          for f in range(NFF):
                              MM(b_, Wv[:, f, :], gT(f), f == 0, f == NFF - 1, wr + R_G(f), bn_)
                          xr = ["x32_%d_%d" % (j, t) for t in range(NT)]
                          STT(xT32[:, j, :], xT32[:, j, :], ALPHA, b_, ALU.mult, ALU.add, bn_ + xr, xr)

                      tasks.append(Task([(0, [128, NFF, 128], "w_down", l, j * 128, (j + 1) * 128)], dn_compute))
                  run_tasks()
                  check('F')
                  p.tag = 'LN2'
                  layer_norm_phase(l, V_LN2G, V_LN2B, final_tok0=tok0 if last else None)
        except _Stop:
            pass
        p.emit()
    build_program.last_prog = p
    build_program.plan = plan_out
    return nc


def build(L=NL, NST=8, st_per_seq=4, stop=None):
    build_program(L, NST, st_per_seq, stop=stop, plan=None)
    return build_program(L, NST, st_per_seq, stop=stop, plan=build_program.plan)


def _chunks(v):
    return np.ascontiguousarray(np.asarray(v, np.float32).reshape(-1, 128).T)


def host_layout(inp, L=NL):
    vecs = np.zeros((128, NL * NV), np.float32)
    for l in range(L):
        o = l * NV
        vecs[:, o + V_BIN:o + V_BIN + 36] = _chunks(inp["b_in"][l])
        vecs[:, o + V_CONVB:o + V_CONVB + 4] = _chunks(inp["conv_b"][l])
        vecs[:, o + V_CLNG:o + V_CLNG + 4] = _chunks(inp["conv_ln_g"][l])
        vecs[:, o + V_CLNB:o + V_CLNB + 4] = _chunks(inp["conv_ln_b"][l])
        vecs[:, o + V_LN1G:o + V_LN1G + 8] = _chunks(inp["ln1_g"][l])
        vecs[:, o + V_LN1B:o + V_LN1B + 8] = _chunks(inp["ln1_b"][l])
        vecs[:, o + V_LN2G:o + V_LN2G + 8] = _chunks(inp["ln2_g"][l])
        vecs[:, o + V_LN2B:o + V_LN2B + 8] = _chunks(inp["ln2_b"][l])
        vecs[:, o + V_FCB:o + V_FCB + NFF] = _chunks(inp["ffn_conv_b"][l])
        for k in range(3):
            vecs[:, o + V_FCW + k * NFF:o + V_FCW + (k + 1) * NFF] = _chunks(inp["ffn_conv_w"][l][k])
        cw = np.asarray(inp["conv_w"][l], np.float32)
        for j in range(4):
            vecs[:, o + V_CW + j * 31:o + V_CW + (j + 1) * 31] = cw[:, j * 128:(j + 1) * 128].T
    kj = np.arange(128)[:, None, None]
    pp = np.arange(5)[None, :, None]
    qi = np.arange(128)[None, None, :]
    rel = 128 * (4 - pp) + qi - kj
    idx = np.clip(rel, -63, 256) + 63
    qc_minus_kc = 2 * (4 - pp) + (qi >= 64).astype(np.int64) - (kj >= 64).astype(np.int64)
    valid = (qc_minus_kc >= 0) & (qc_minus_kc <= 8)
    rb = np.asarray(inp["rel_bias"], np.float32)
    bias = np.empty((NL, 128, 8, 5, 128), np.float32)
    for l in range(NL):
        g = rb[l][:, idx]
        g = np.where(valid[None], g, np.float32(NEG))
        bias[l] = g.transpose(1, 0, 2, 3)
    bias = bias.reshape(NL, 128, 8 * 640)
    lnf = np.stack([np.broadcast_to(np.asarray(inp["ln2_g"][L - 1], np.float32), (128, D)),
                    np.broadcast_to(np.asarray(inp["ln2_b"][L - 1], np.float32), (128, D))])
    shared = {
        "w_in": np.ascontiguousarray(inp["w_in"], np.float32),
        "w_att_out": np.ascontiguousarray(inp["w_att_out"], np.float32),
        "w_conv_out": np.ascontiguousarray(inp["w_conv_out"], np.float32),
        "w_o": np.ascontiguousarray(inp["w_o"], np.float32),
        "w_up": np.ascontiguousarray(inp["w_up"], np.float32),
        "w_down": np.ascontiguousarray(inp["w_down"], np.float32),
        "vecs": vecs,
        "biasT": bias,
        "lnf": np.ascontiguousarray(lnf),
        "ident": np.eye(128, dtype=np.float32),
    }
    return shared


def kernel(**inputs):
    inp = {k: np.asarray(v) for k, v in inputs.items()}
    x = np.ascontiguousarray(inp["x"], np.float32)
    shared = host_layout(inp)
    n = 8
    nc = build(L=NL, NST=8, st_per_seq=4)
    in_maps = []
    for c in range(n):
        m = dict(shared)
        m["x"] = x[2 * c:2 * c + 2].reshape(2 * SEQ, D)
        in_maps.append(m)
    res = run_bass_kernel_spmd(nc, in_maps, core_ids=list(range(n)))
    out = np.concatenate([np.asarray(r["y"], np.float32).reshape(2, SEQ, D) for r in res.results], axis=0)
    return out
```
